# Optimizing a Trainium2 kernel written in Bass

```python
import math
import jax, jax.numpy as jnp
from jax import lax
import numpy as np

D_MODEL = 2048
BATCH = 2
SEQ = 16384
DEPTH = 4

MIX_WIDTH = D_MODEL
POOL_WIDTH = MIX_WIDTH // 2
POOL_WINDOWS = (2, 4, 8, 16)
N_POOL_GROUPS = len(POOL_WINDOWS)
POOL_GROUP = POOL_WIDTH // N_POOL_GROUPS
DIFF_WIDTH = MIX_WIDTH - POOL_WIDTH
DIFF_HEAD_DIM = 64
DIFF_V_DIM = 2 * DIFF_HEAD_DIM
N_DIFF_HEADS = DIFF_WIDTH // DIFF_V_DIM
QK_WIDTH = N_DIFF_HEADS * 2 * DIFF_HEAD_DIM
IN_WIDTH = POOL_WIDTH + 2 * QK_WIDTH + DIFF_WIDTH
D_FF = 5632
ROPE_THETA = 10000.0
Q_BLOCK = 128
RMS_EPS = 1e-6
SUBLN_EPS = 1e-5
N_MOD = 9

kernel_name = "hybrid_pool_diffattn_macaron_adaln"


def rmsnorm(x, g, eps=RMS_EPS):
    xf = x.astype(jnp.float32)
    y = xf * lax.rsqrt(jnp.mean(xf * xf, axis=-1, keepdims=True) + eps)
    return (y * g.astype(jnp.float32)).astype(x.dtype)


def modulate(h, shift, scale):
    return h * (1 + scale) + shift


def swiglu(h, w_in, w_out):
    a = h @ w_in
    gate, up = a[..., :D_FF], a[..., D_FF:]
    return (jax.nn.silu(gate) * up) @ w_out


def rope_tables(seq_len, dim):
    pos = jnp.arange(seq_len, dtype=jnp.float32)
    inv_freq = ROPE_THETA ** (-jnp.arange(0, dim, 2, dtype=jnp.float32) / dim)
    ang = pos[:, None] * inv_freq[None, :]
    return jnp.cos(ang), jnp.sin(ang)


def apply_rope(t, cos, sin):
    half = t.shape[-1] // 2
    c = cos[None, :, None, None, :].astype(t.dtype)
    s = sin[None, :, None, None, :].astype(t.dtype)
    t1, t2 = t[..., :half], t[..., half:]
    return jnp.concatenate([t1 * c - t2 * s, t1 * s + t2 * c], axis=-1)


def pool_mixer(p, w, b, scale):
    B, S, _ = p.shape
    pf = p.astype(jnp.float32)
    cs = jnp.pad(jnp.cumsum(pf, axis=1), ((0, 0), (1, 0), (0, 0)))
    t = jnp.arange(S)
    groups = []
    for gi, win in enumerate(POOL_WINDOWS):
        lo, hi = gi * POOL_GROUP, (gi + 1) * POOL_GROUP
        cs_g = cs[..., lo:hi]
        start = jnp.maximum(t + 1 - win, 0)
        win_sum = cs_g[:, 1:] - cs_g[:, start]
        count = (t + 1 - start).astype(jnp.float32)
        groups.append(win_sum / count[None, :, None] - pf[..., lo:hi])
    pooled = jnp.stack(groups, axis=2).astype(p.dtype)
    mixed = jnp.einsum('bsgc,gcd->bsgd', pooled, w) + b
    return mixed.reshape(B, S, POOL_WIDTH) * scale


def diff_attention(q, k, v, lam, subln_g, lambda_init):
    B, S, H, _, dh = q.shape
    dv = v.shape[-1]
    n_blocks = S // Q_BLOCK
    qb = q.reshape(B, n_blocks, Q_BLOCK, H, 2, dh).transpose(1, 0, 2, 3, 4, 5)
    sm_scale = dh ** -0.5
    key_pos = jnp.arange(S)
    neg = jnp.finfo(jnp.float32).min

    def one_block(args):
        qi, bi = args
        s = jnp.einsum('bqhcd,bkhcd->bhcqk', qi, k).astype(jnp.float32) * sm_scale
        q_pos = bi * Q_BLOCK + jnp.arange(Q_BLOCK)
        mask = key_pos[None, :] <= q_pos[:, None]
        s = jnp.where(mask, s, neg)
        pm = jax.nn.softmax(s, axis=-1)
        a = pm[:, :, 0] - lam * pm[:, :, 1]
        return jnp.einsum('bhqk,bkhd->bqhd', a.astype(v.dtype), v)

    out = lax.map(one_block, (qb, jnp.arange(n_blocks)))
    out = out.transpose(1, 0, 2, 3, 4).reshape(B, S, H, dv)
    out = rmsnorm(out, subln_g, SUBLN_EPS) * (1.0 - lambda_init)
    return out.reshape(B, S, H * dv)


def setup_inputs(seed: int = 0) -> dict:
    key = jax.random.key(seed)
    ks = jax.random.split(key, 24)
    f32 = jnp.float32
    nrm = lambda k, shape, s: jax.random.normal(k, shape, f32) * s
    L, D, F = DEPTH, D_MODEL, D_FF
    return {
        "x": nrm(ks[0], (BATCH, SEQ, D), 1.0),
        "c": nrm(ks[1], (BATCH, D), 1.0),
        "w_mod": nrm(ks[2], (L, D, N_MOD * D), 0.5 * D ** -0.5),
        "b_mod": nrm(ks[3], (L, N_MOD * D), 0.02),
        "norm_ffn1": 1.0 + nrm(ks[4], (L, D), 0.05),
        "ffn1_w_in": nrm(ks[5], (L, D, 2 * F), D ** -0.5),
        "ffn1_w_out": nrm(ks[6], (L, F, D), F ** -0.5),
        "norm_mix": 1.0 + nrm(ks[7], (L, D), 0.05),
        "w_in": nrm(ks[8], (L, D, IN_WIDTH), D ** -0.5),
        "pool_w": nrm(ks[9], (L, N_POOL_GROUPS, POOL_GROUP, POOL_GROUP), POOL_GROUP ** -0.5),
        "pool_b": nrm(ks[10], (L, N_POOL_GROUPS, POOL_GROUP), 0.02),
        "pool_scale": 1.0 + nrm(ks[11], (L, POOL_WIDTH), 0.1),
        "diff_lambda": nrm(ks[12], (L, 4, DIFF_HEAD_DIM), 0.1),
        "diff_subln": 1.0 + nrm(ks[13], (L, DIFF_V_DIM), 0.05),
        "w_out": nrm(ks[14], (L, MIX_WIDTH, D), MIX_WIDTH ** -0.5),
        "norm_ffn2": 1.0 + nrm(ks[15], (L, D), 0.05),
        "ffn2_w_in": nrm(ks[16], (L, D, 2 * F), D ** -0.5),
        "ffn2_w_out": nrm(ks[17], (L, F, D), F ** -0.5),
        "final_norm": 1.0 + nrm(ks[18], (D,), 0.05),
    }


def reference(x, c, w_mod, b_mod, norm_ffn1, ffn1_w_in, ffn1_w_out, norm_mix, w_in,
              pool_w, pool_b, pool_scale, diff_lambda, diff_subln, w_out,
              norm_ffn2, ffn2_w_in, ffn2_w_out, final_norm):
    B, S, D = x.shape
    cos, sin = rope_tables(S, DIFF_HEAD_DIM)
    c_act = jax.nn.silu(c)
    for l in range(DEPTH):
        lambda_init = 0.8 - 0.6 * math.exp(-0.3 * l)
        mod = (c_act @ w_mod[l] + b_mod[l]).reshape(B, N_MOD, 1, D)
        sh1, sc1, g1, sh2, sc2, g2, sh3, sc3, g3 = [mod[:, i] for i in range(N_MOD)]

        h = modulate(rmsnorm(x, norm_ffn1[l]), sh1, sc1)
        x = x + 0.5 * g1 * swiglu(h, ffn1_w_in[l], ffn1_w_out[l])

        h = modulate(rmsnorm(x, norm_mix[l]), sh2, sc2)
        z = h @ w_in[l]
        o1, o2, o3 = POOL_WIDTH, POOL_WIDTH + QK_WIDTH, POOL_WIDTH + 2 * QK_WIDTH
        p_in = z[..., :o1]
        q = apply_rope(z[..., o1:o2].reshape(B, S, N_DIFF_HEADS, 2, DIFF_HEAD_DIM), cos, sin)
        k = apply_rope(z[..., o2:o3].reshape(B, S, N_DIFF_HEADS, 2, DIFF_HEAD_DIM), cos, sin)
        v = z[..., o3:].reshape(B, S, N_DIFF_HEADS, DIFF_V_DIM)

        lam_p = diff_lambda[l].astype(jnp.float32)
        lam = (jnp.exp(jnp.sum(lam_p[0] * lam_p[1])) - jnp.exp(jnp.sum(lam_p[2] * lam_p[3]))
               + lambda_init)

        y_pool = pool_mixer(p_in, pool_w[l], pool_b[l], pool_scale[l])
        y_attn = diff_attention(q, k, v, lam, diff_subln[l], lambda_init)
        y = jnp.concatenate([y_pool, y_attn], axis=-1) @ w_out[l]
        x = x + g2 * y

        h = modulate(rmsnorm(x, norm_ffn2[l]), sh3, sc3)
        x = x + 0.5 * g3 * swiglu(h, ffn2_w_in[l], ffn2_w_out[l])
    return rmsnorm(x, final_norm)
```

```python
import math
from contextlib import ExitStack

import ml_dtypes
import numpy as np

import concourse.bass as bass
import concourse.mybir as mybir
from concourse.bass_utils import run_bass_kernel_spmd

F32 = mybir.dt.float32
BF16 = mybir.dt.bfloat16
AF = mybir.ActivationFunctionType
ALU = mybir.AluOpType
AX = mybir.AxisListType

NCORE = 8
POOL_WINDOWS = (2, 4, 8, 16)
RMS_EPS = 1e-6
SUBLN_EPS = 1e-5


class Cfg:
    def __init__(self, D=2048, F=5632, S=16384, B=2, L=4, T=512):
        self.D, self.F, self.S, self.B, self.L, self.T = D, F, S, B, L, T
        self.DC = D // 128
        self.NF = F // 128
        self.TOK = B * S // NCORE
        self.NT = self.TOK // T
        self.NQT = S // 512
        assert D == 2048 and self.TOK % T == 0 and S % 2048 == 0


class DSem:
    def __init__(self, h):
        self.h = h
        self.count = 0


class Prog:
    ENG = ("pe", "act", "dve", "pool", "sp")

    def __init__(self, nc):
        self.nc = nc
        self.es = ExitStack()
        self.ops = {e: [] for e in self.ENG}
        self.pg = {e: self.es.enter_context(nc.semaphore("pg_" + e)) for e in ("pe", "act", "dve")}
        self.cnt = {e: 0 for e in self.ENG}
        self.waited = {}
        self.nsem = 0

    def dsem(self, name):
        self.nsem += 1
        return DSem(self.es.enter_context(self.nc.semaphore(f"{name}_{self.nsem}")))

    def sb(self, name, shape, dt):
        return self.es.enter_context(self.nc.sbuf_tensor(name, shape, dt))

    def ps(self, name, shape, dt=F32):
        return self.es.enter_context(self.nc.psum_tensor(name, shape, dt))

    def do(self, eng, fn, sig=False):
        if sig:
            self.cnt[eng] += 1
            c = self.cnt[eng]
            sem = self.pg[eng]
            self.ops[eng].append(lambda e, fn=fn, sem=sem: fn(e).then_inc(sem, 1))
            return (sem, c)
        self.ops[eng].append(fn)
        return None

    def wait(self, eng, tok):
        if tok is None:
            return
        sem, c = tok
        if c <= 0:
            return
        key = (eng, id(sem))
        if self.waited.get(key, 0) >= c:
            return
        self.waited[key] = c
        self.ops[eng].append(lambda e, sem=sem, c=c: e.wait_ge(sem, c))

    def dma(self, eng, out, in_, ds):
        ds.count += 16
        self.ops[eng].append(lambda e, out=out, in_=in_, h=ds.h: e.dma_start(out=out, in_=in_).then_inc(h, 16))
        return (ds.h, ds.count)

    def mm(self, out, lhsT, rhs, start, stop, sig=False):
        return self.do("pe", lambda e: e.matmul(out, lhsT=lhsT, rhs=rhs, start=start, stop=stop), sig)

    def act(self, out, in_, func, scale=1.0, bias=0.0, sig=False):
        return self.do("act", lambda e: e.activation(out=out, in_=in_, func=func, bias=bias, scale=scale), sig)

    def tt(self, out, in0, in1, op, sig=False, eng="dve"):
        return self.do(eng, lambda e: e.tensor_tensor(out=out, in0=in0, in1=in1, op=op), sig)

    def ts(self, out, in0, s1, s2, op0, op1=None, sig=False, eng="dve"):
        if op1 is None:
            return self.do(eng, lambda e: e.tensor_scalar(out=out, in0=in0, scalar1=s1, scalar2=None, op0=op0), sig)
        return self.do(eng, lambda e: e.tensor_scalar(out=out, in0=in0, scalar1=s1, scalar2=s2, op0=op0, op1=op1), sig)

    def stt(self, out, in0, scalar, in1, op0, op1, sig=False, eng="dve"):
        return self.do(eng, lambda e: e.scalar_tensor_tensor(out=out, in0=in0, scalar=scalar, in1=in1, op0=op0, op1=op1), sig)

    def copy(self, eng, out, in_, sig=False):
        if eng == "act":
            return self.do("act", lambda e: e.activation(out=out, in_=in_, func=AF.Copy), sig)
        return self.do(eng, lambda e: e.tensor_copy(out=out, in_=in_), sig)

    def memset(self, eng, ap, val, sig=False):
        return self.do(eng, lambda e: e.memset(ap, val), sig)

    def emit(self):
        nc = self.nc
        with nc.Block() as block:
            @block.tensor
            def _(e):
                for f in self.ops["pe"]:
                    f(e)

            @block.scalar
            def _(e):
                for f in self.ops["act"]:
                    f(e)

            @block.vector
            def _(e):
                for f in self.ops["dve"]:
                    f(e)

            @block.gpsimd
            def _(e):
                for f in self.ops["pool"]:
                    f(e)

            @block.sync
            def _(e):
                for f in self.ops["sp"]:
                    f(e)
        self.es.close()


class WRing:
    def __init__(self, P, name, shape, nbuf, prep_tok_fn):
        self.P = P
        self.nbuf = nbuf
        self.bufs = [P.sb(f"{name}{i}", shape, BF16) for i in range(nbuf)]
        self.ld = [P.dsem(f"{name}ld{i}") for i in range(nbuf)]
        self.free_tok = [None] * nbuf
        self.k_load = 0
        self.k_rel = 0
        self.reqs = []
        self.pending = []
        self.prep_tok_fn = prep_tok_fn

    def load(self, src_ap):
        self.reqs.append(src_ap)
        self._pump()

    def _pump(self):
        P = self.P
        while self.reqs and self.k_load - self.k_rel < self.nbuf:
            src_ap = self.reqs.pop(0)
            b = self.k_load % self.nbuf
            P.wait("sp", self.prep_tok_fn())
            P.wait("sp", self.free_tok[b])
            tok = P.dma("sp", self.bufs[b][:], src_ap, self.ld[b])
            self.pending.append((b, tok))
            self.k_load += 1

    def acquire(self):
        b, tok = self.pending.pop(0)
        self.P.wait("pe", tok)
        return b, self.bufs[b]

    def release(self, b, pe_tok):
        self.free_tok[b] = pe_tok
        self.k_rel += 1
        self._pump()


def _prep_cast(P, prep, dst, src):
    P.dma("pool", dst, src, prep)


def _emit_mod(P, cfg, nmod, cT, wmod, bmodT, modv, stage, y_ps, ld_sem):
    DC = cfg.DC
    cact, ct_raw, bm, sg = stage["cact"], stage["ct_raw"], stage["bm"], stage["sg"]
    t1 = P.dma("pool", ct_raw[:], cT, ld_sem)
    t2 = P.dma("pool", bm[:], bmodT, ld_sem)
    P.wait("act", t2)
    a_tok = P.act(sg[:], ct_raw[:], AF.Sigmoid, sig=True)
    P.wait("dve", a_tok)
    c_tok = P.tt(cact[:], ct_raw[:], sg[:], ALU.mult, sig=True)
    wbufs = stage["wbufs"]
    wsem = [P.dsem("wmod0"), P.dsem("wmod1")]
    free = [None, None]
    nch = nmod * DC
    ntile = nch // 2
    wsrc = wmod.rearrange("(dc p) n -> p dc n", p=128)
    P.wait("pe", c_tok)
    last = None
    for t in range(ntile):
        b = t % 2
        P.wait("pool", free[b])
        lt = P.dma("pool", wbufs[b], wsrc[:, :, t * 256:(t + 1) * 256], wsem[b])
        P.wait("pe", lt)
        for half in range(2):
            j = 2 * t + half
            for dc in range(DC):
                last = P.mm(y_ps[:, j:j + 1], wbufs[b][:, dc, half * 128:(half + 1) * 128], cact[:, dc:dc + 1],
                            start=(dc == 0), stop=(dc == DC - 1), sig=(dc == DC - 1 and half == 1))
        free[b] = last
    P.wait("dve", last)
    return P.tt(modv[:, 0:nch], y_ps[:, 0:nch], bm[:, 0:nch], ALU.add, sig=True)


def _emit_norm(P, cfg, xt, sq, ss_ps, rstd, ones_bf, x_tok, eps):
    DC = cfg.DC
    P.wait("act", x_tok)
    pe_tok = None
    for q in range(4):
        lo, hi = q * DC // 4, (q + 1) * DC // 4
        a_tok = P.act(sq[:, lo:hi, :], xt[:, lo:hi, :], AF.Square, sig=True)
        P.wait("pe", a_tok)
        for dc in range(lo, hi):
            pe_tok = P.mm(ss_ps[:, :], ones_bf[:, :], sq[:, dc, :], start=(dc == 0), stop=(dc == DC - 1),
                          sig=(dc == DC - 1))
    P.wait("dve", pe_tok)
    d_tok = P.ts(rstd[:, :], ss_ps[:, :], 1.0 / cfg.D, eps, ALU.mult, ALU.add, sig=True)
    P.wait("act", d_tok)
    s_tok = P.act(rstd[:, :], rstd[:, :], AF.Sqrt, sig=True)
    P.wait("dve", s_tok)
    P.do("dve", lambda e: e.reciprocal(out=rstd[:, :], in_=rstd[:, :]))
    return s_tok


def _emit_h(P, cfg, xt, rstd, Av, Bv, hT, tmp, hT_free_tok):
    P.wait("dve", hT_free_tok)
    tok = None
    for dc in range(cfg.DC):
        P.tt(tmp[:, :], xt[:, dc, :], rstd[:, :], ALU.mult)
        tok = P.ts(hT[:, dc, :], tmp[:, :], Av[:, dc:dc + 1], Bv[:, dc:dc + 1], ALU.mult, ALU.add, sig=(dc == cfg.DC - 1))
    return tok


def _emit_ffn(P, cfg, st, W1src, W2src, Av, Bv, Gv, x_tok):
    DC, NF = cfg.DC, cfg.NF
    xt, hT, gT, rstd = st["xt"], st["hT"], st["gT"], st["rstd"]
    sq = st["sq"]
    for j in range(NF):
        st["w1"].load(W1src(j))
    for dcn in range(DC):
        st["w2"].load(W2src(dcn))
    _emit_norm(P, cfg, xt, sq, st["ss_ps"], rstd, st["ones_bf"], x_tok, RMS_EPS)
    h_tok = _emit_h(P, cfg, xt, rstd, Av, Bv, hT, st["tmp"], st["hT_free"])
    P.wait("pe", h_tok)
    gu = st["gu_ps"]
    gtok = None
    for j in range(NF):
        b, wt = st["w1"].acquire()
        s = j % 2
        g_ps, u_ps = gu[2 * s], gu[2 * s + 1]
        P.wait("pe", st["gu_free"][s])
        for dc in range(DC):
            P.mm(g_ps[:, :], wt[:, dc, 0:128], hT[:, dc, :], start=(dc == 0), stop=(dc == DC - 1))
        pt = None
        for dc in range(DC):
            pt = P.mm(u_ps[:, :], wt[:, dc, 128:256], hT[:, dc, :], start=(dc == 0), stop=(dc == DC - 1),
                      sig=(dc == DC - 1))
        st["w1"].release(b, pt)
        if j == NF - 1:
            st["hT_free"] = pt
        sgb = st["sgb"][s]
        P.wait("act", pt)
        P.wait("act", st["sg_free"][s])
        at = P.act(sgb[:, :], g_ps[:, :], AF.Silu, sig=True)
        P.wait("dve", at)
        gtok = P.tt(gT[:, j, :], sgb[:, :], u_ps[:, :], ALU.mult, sig=True)
        st["sg_free"][s] = gtok
        st["gu_free"][s] = gtok
    P.wait("pe", gtok)
    xtok = None
    for dcn in range(DC):
        b, wt = st["w2"].acquire()
        s = dcn % 2
        y_ps = st["y_ps"][s]
        P.wait("pe", st["y_free"][s])
        pt = None
        for fc in range(NF):
            pt = P.mm(y_ps[:, :], wt[:, fc, :], gT[:, fc, :], start=(fc == 0), stop=(fc == NF - 1), sig=(fc == NF - 1))
        st["w2"].release(b, pt)
        P.wait("dve", pt)
        xtok = P.stt(xt[:, dcn, :], y_ps[:, :], Gv[:, dcn:dcn + 1], xt[:, dcn, :], ALU.mult, ALU.add, sig=True)
        st["y_free"][s] = xtok
    return xtok


def _alloc_common(P, cfg):
    T, DC, NF = cfg.T, cfg.DC, cfg.NF
    st = {}
    st["xt"] = P.sb("xt", [128, DC, T], F32)
    st["hT"] = P.sb("hT", [128, DC, T], BF16)
    NG = max(NF, DC)
    st["gT"] = P.sb("gT", [128, NG, T], BF16)
    st["sq"] = st["gT"]
    st["rstd"] = P.sb("rstd", [128, T], F32)
    st["tmp"] = P.sb("tmp", [128, T], F32)
    st["sgb"] = [P.sb(f"sgb{i}", [128, T], F32) for i in range(2)]
    st["ones_bf"] = P.sb("ones_bf", [128, 128], BF16)
    st["ss_ps"] = P.ps("ss_ps", [128, T])
    st["gu_ps"] = [P.ps(f"gu_ps{i}", [128, T]) for i in range(4)]
    st["y_ps"] = [P.ps(f"y_ps{i}", [128, T]) for i in range(2)]
    st["gu_free"] = [None, None]
    st["sg_free"] = [None, None]
    st["y_free"] = [None, None]
    st["hT_free"] = None
    return st


def build_p1(cfg):
    T, DC, NF, TOK, NT, D, F = cfg.T, cfg.DC, cfg.NF, cfg.TOK, cfg.NT, cfg.D, cfg.F
    nc = bass.Bass("TRN2", target_bir_lowering=False)
    dt = lambda n, s, d, k: nc.dram_tensor(n, s, d, kind=k).ap()
    xT = dt("xT", [D, TOK], F32, "ExternalInput")
    cT = dt("cT", [128, DC], F32, "ExternalInput")
    wmod = dt("wmod", [D, 5 * D], F32, "ExternalInput")
    bmodT = dt("bmodT", [128, 5 * DC], F32, "ExternalInput")
    ng1T = dt("ng1T", [128, DC], F32, "ExternalInput")
    ng2T = dt("ng2T", [128, DC], F32, "ExternalInput")
    fw_in = dt("fw_in", [D, 2 * F], F32, "ExternalInput")
    fw_out = dt("fw_out", [F, D], F32, "ExternalInput")
    w_in = dt("w_in", [D, 4096], F32, "ExternalInput")
    pool_w = dt("pool_w", [4, 256, 256], F32, "ExternalInput")
    ropeC = dt("ropeC", [128, TOK], F32, "ExternalInput")
    ropeS = dt("ropeS", [128, TOK], F32, "ExternalInput")
    permM = dt("permM", [128, 128], F32, "ExternalInput")
    xT_out = dt("xT_out", [D, TOK], F32, "ExternalOutput")
    QT = dt("QT", [1024, TOK], BF16, "ExternalOutput")
    KT = dt("KT", [1024, TOK], BF16, "ExternalOutput")
    Vo = dt("Vo", [TOK, 1024], BF16, "ExternalOutput")
    PMT = dt("PMT", [1024, TOK], BF16, "ExternalOutput")
    wb1 = nc.dram_tensor("wb1", [NF, 128, DC, 256], BF16).ap()
    wb2 = nc.dram_tensor("wb2", [DC, 128, NF, 128], BF16).ap()
    wbi = nc.dram_tensor("wbi", [16, 128, DC, 256], BF16).ap()
    wbp = nc.dram_tensor("wbp", [128, 4, 2, 256], BF16).ap()

    P = Prog(nc)
    st = _alloc_common(P, cfg)
    prep = P.dsem("prep")
    misc = P.dsem("misc")
    fw_in_v = fw_in.rearrange("(dc p) (two j c) -> p dc two j c", p=128, two=2, c=128)
    for j in range(NF):
        for two in range(2):
            _prep_cast(P, prep, wb1[j][:, :, two * 128:(two + 1) * 128], fw_in_v[:, :, two, j, :])
    fw_out_v = fw_out.rearrange("(fc p) (dn c) -> p fc dn c", p=128, c=128)
    for dcn in range(DC):
        _prep_cast(P, prep, wb2[dcn], fw_out_v[:, :, dcn, :])
    w_in_v = w_in.rearrange("(dc p) (t c) -> p dc t c", p=128, c=256)
    for t in range(16):
        _prep_cast(P, prep, wbi[t], w_in_v[:, :, t, :])
    _prep_cast(P, prep, wbp, pool_w.rearrange("g (cc p) n -> p g cc n", p=128))
    prep_total = [None]
    prep_total[0] = (prep.h, prep.count)
    prep_tok = lambda: prep_total[0]

    st["w1"] = WRing(P, "w1b", [128, DC, 256], 3, prep_tok)
    st["w2"] = WRing(P, "w2b", [128, NF, 128], 2, prep_tok)
    modv = P.sb("modv", [128, 5 * DC], F32)
    ng1 = P.sb("ng1", [128, DC], F32)
    ng2 = P.sb("ng2", [128, DC], F32)
    A1 = P.sb("A1", [128, DC], F32)
    A2 = P.sb("A2", [128, DC], F32)
    G1 = P.sb("G1", [128, DC], F32)
    pm_f = P.sb("pm_f", [128, 128], F32)
    pw_sb = P.sb("pw_sb", [128, 4, 2, 256], BF16)
    rC = P.sb("rC", [128, T], F32)
    rS = P.sb("rS", [128, T], F32)
    pinT = P.sb("pinT", [128, 8, T], BF16)
    zf = [P.sb(f"zf{i}", [128, T], F32) for i in range(2)]
    t1 = P.sb("t1", [128, T], F32)
    t2 = P.sb("t2", [128, T], F32)
    ob = [P.sb(f"ob{i}", [128, T], BF16) for i in range(2)]
    vb = P.sb("vb", [128, 4, 1024], BF16)
    pmb = [P.sb(f"pmb{i}", [128, T], BF16) for i in range(2)]
    stage = {
        "cact": P.sb("cact", [128, DC], F32), "ct_raw": P.sb("ct_raw", [128, DC], F32),
        "bm": P.sb("bm", [128, 5 * DC], F32), "sg": P.sb("sgc", [128, DC], F32),
    }
    stage["wbufs"] = [st["xt"][:, :, 0:256], st["xt"][:, :, 256:512]]

    P.memset("dve", st["ones_bf"][:, :], 1.0)
    P.dma("pool", ng1[:], ng1T, misc)
    P.dma("pool", ng2[:], ng2T, misc)
    P.dma("pool", pm_f[:], permM, misc)
    mod_tok = _emit_mod(P, cfg, 5, cT, wmod, bmodT, modv, stage, st["y_ps"][0], misc)
    P.wait("pool", prep_tok())
    P.dma("pool", pw_sb[:], wbp, misc)
    misc_tok = (misc.h, misc.count)
    P.wait("dve", misc_tok)
    P.wait("dve", mod_tok)
    sh1, sc1, g1, sh2, sc2 = [modv[:, i * DC:(i + 1) * DC] for i in range(5)]
    P.stt(A1[:], sc1, 1.0, ng1[:], ALU.add, ALU.mult)
    P.stt(A2[:], sc2, 1.0, ng2[:], ALU.add, ALU.mult)
    const_tok = P.ts(G1[:], g1, 0.5, None, ALU.mult, sig=True)
    P.wait("pe", misc_tok)
    P.wait("act", misc_tok)

    xld = P.dsem("xld")
    rld = P.dsem("rld")
    xst = P.dsem("xst")
    ost = P.dsem("ost")
    xT_v = xT.rearrange("(dc p) t -> p dc t", p=128)
    xTo_v = xT_out.rearrange("(dc p) t -> p dc t", p=128)
    xt = st["xt"]
    hT = st["hT"]
    xt_free = [mod_tok]
    rope_free = None
    z_free = [None, None]
    r_free = [None, None]
    zf_free = [None, None]
    ob_free = [None, None]
    vb_free = None
    pmb_free = [None, None]
    pin_free = None
    for it in range(NT):
        c0, c1 = it * T, (it + 1) * T
        for tk in xt_free:
            P.wait("pool", tk)
        x_ld = P.dma("pool", xt[:], xT_v[:, :, c0:c1], xld)
        P.wait("pool", rope_free)
        P.dma("pool", rC[:], ropeC[:, c0:c1], rld)
        rope_tok = P.dma("pool", rS[:], ropeS[:, c0:c1], rld)
        P.wait("act", x_ld)
        P.wait("dve", x_ld)
        P.wait("dve", const_tok)
        st["gu_free"] = [z_free[0] if z_free[0] else st["gu_free"][0], r_free[0] if r_free[0] else st["gu_free"][1]]
        P.wait("pe", z_free[1])
        P.wait("pe", r_free[1])
        xtok = _emit_ffn(P, cfg, st, lambda j: wb1[j], lambda d: wb2[d], A1, sh1, G1, x_ld)
        P.wait("pool", xtok)
        xs_tok = P.dma("pool", xTo_v[:, :, c0:c1], xt[:], xst)
        for t in range(16):
            st["w1"].load(wbi[t])
        sq_tok = _emit_norm(P, cfg, xt, st["sq"], st["ss_ps"], st["rstd"], st["ones_bf"], xtok, RMS_EPS)
        h_tok = _emit_h(P, cfg, xt, st["rstd"], A2, sh2, hT, st["tmp"], st["hT_free"])
        xt_free = [xs_tok, h_tok, sq_tok]
        P.wait("pe", h_tok)
        P.wait("dve", rope_tok)
        zr = st["gu_ps"]
        z_ring = [zr[0], zr[1]]
        r_ring = [zr[2], zr[3]]
        z_free = [st["gu_free"][0], st["gu_free"][0]]
        r_free = [st["gu_free"][1], st["gu_free"][1]]
        pend = None
        nz = 0

        def finish_rope(pend):
            cc, zi, a_tok = pend
            ri = zi
            P.wait("pe", a_tok)
            P.wait("pe", r_free[ri])
            pt = P.mm(r_ring[ri][:, :], pm_f[:, :], zf[zi][:, :], start=True, stop=True, sig=True)
            P.wait("dve", pt)
            P.tt(t1[:, :], zf[zi][:, :], rC[:, :], ALU.mult)
            P.tt(t2[:, :], r_ring[ri][:, :], rS[:, :], ALU.mult)
            oi = cc % 2
            P.wait("dve", ob_free[oi])
            dtok = P.tt(ob[oi][:, :], t1[:, :], t2[:, :], ALU.add, sig=True)
            r_free[ri] = dtok
            zf_free[zi] = dtok
            dst = QT if cc < 16 else KT
            hh = (cc - 8) % 8
            P.wait("pool", dtok)
            ob_free[oi] = P.dma("pool", dst[hh * 128:(hh + 1) * 128, c0:c1], ob[oi][:, :], ost)
            return dtok

        P.wait("act", pin_free)
        last_rope = None
        for wt_i in range(12):
            b, wt = st["w1"].acquire()
            pt = None
            for half in range(2):
                cc = 2 * wt_i + half
                zi = nz % 2
                nz += 1
                P.wait("pe", z_free[zi])
                for dc in range(DC):
                    pt = P.mm(z_ring[zi][:, :], wt[:, dc, half * 128:(half + 1) * 128], hT[:, dc, :],
                              start=(dc == 0), stop=(dc == DC - 1), sig=(dc == DC - 1))
                P.wait("act", pt)
                if cc < 8:
                    a_tok = P.copy("act", pinT[:, cc, :], z_ring[zi][:, :], sig=True)
                    z_free[zi] = a_tok
                    pin_tok = a_tok
                else:
                    P.wait("act", zf_free[zi])
                    a_tok = P.copy("act", zf[zi][:, :], z_ring[zi][:, :], sig=True)
                    z_free[zi] = a_tok
                    if pend is not None:
                        last_rope = finish_rope(pend)
                    pend = (cc, zi, a_tok)
            st["w1"].release(b, pt)
        last_rope = finish_rope(pend)
        rope_free = last_rope
        P.wait("act", vb_free)
        vtok = None
        for vt in range(4):
            b, wt = st["w1"].acquire()
            pt = None
            for ts_ in range(4):
                zi = nz % 2
                nz += 1
                P.wait("pe", z_free[zi])
                for dc in range(DC):
                    pt = P.mm(z_ring[zi][:, 0:256], hT[:, dc, ts_ * 128:(ts_ + 1) * 128], wt[:, dc, :],
                              start=(dc == 0), stop=(dc == DC - 1), sig=(dc == DC - 1))
                P.wait("act", pt)
                vtok = P.copy("act", vb[:, ts_, vt * 256:(vt + 1) * 256], z_ring[zi][:, 0:256], sig=True)
                z_free[zi] = vtok
            st["w1"].release(b, pt)
        st["hT_free"] = pt
        P.wait("pool", vtok)
        vb_free = P.dma("pool", Vo[c0:c1, :].rearrange("(s p) c -> p s c", p=128), vb[:], ost)
        P.wait("pe", pin_tok)
        for n in range(8):
            g, nn = n // 2, n % 2
            zi = nz % 2
            nz += 1
            P.wait("pe", z_free[zi])
            pt = None
            for cc in range(2):
                pt = P.mm(z_ring[zi][:, :], pw_sb[:, g, cc, nn * 128:(nn + 1) * 128], pinT[:, 2 * g + cc, :],
                          start=(cc == 0), stop=(cc == 1), sig=(cc == 1))
            pin_free = pt
            oi = n % 2
            P.wait("act", pt)
            P.wait("act", pmb_free[oi])
            a_tok = P.copy("act", pmb[oi][:, :], z_ring[zi][:, :], sig=True)
            z_free[zi] = a_tok
            P.wait("pool", a_tok)
            pmb_free[oi] = P.dma("pool", PMT[n * 128:(n + 1) * 128, c0:c1], pmb[oi][:, :], ost)
    P.do("pool", lambda e: e.wait_ge(xst.h, xst.count))
    P.do("pool", lambda e: e.wait_ge(ost.h, ost.count))
    P.emit()
    return nc


def build_p3(cfg):
    T, DC, NF, TOK, NT, D, F = cfg.T, cfg.DC, cfg.NF, cfg.TOK, cfg.NT, cfg.D, cfg.F
    nc = bass.Bass("TRN2", target_bir_lowering=False)
    dt = lambda n, s, d, k: nc.dram_tensor(n, s, d, kind=k).ap()
    xT = dt("xT", [D, TOK], F32, "ExternalInput")
    catT = dt("catT", [D, TOK], BF16, "ExternalInput")
    cT = dt("cT", [128, DC], F32, "ExternalInput")
    wmod = dt("wmod", [D, 4 * D], F32, "ExternalInput")
    bmodT = dt("bmodT", [128, 4 * DC], F32, "ExternalInput")
    ng3T = dt("ng3T", [128, DC], F32, "ExternalInput")
    fnT = dt("fnT", [128, DC], F32, "ExternalInput")
    fw_in = dt("fw_in", [D, 2 * F], F32, "ExternalInput")
    fw_out = dt("fw_out", [F, D], F32, "ExternalInput")
    w_out = dt("w_out", [D, D], F32, "ExternalInput")
    xT_out = dt("xT_out", [D, TOK], F32, "ExternalOutput")
    oT = dt("oT", [D, TOK], F32, "ExternalOutput")
    wb1 = nc.dram_tensor("wb1", [NF, 128, DC, 256], BF16).ap()
    wb2 = nc.dram_tensor("wb2", [DC, 128, NF, 128], BF16).ap()
    wbo = nc.dram_tensor("wbo", [8, 128, DC, 256], BF16).ap()

    P = Prog(nc)
    st = _alloc_common(P, cfg)
    prep = P.dsem("prep")
    misc = P.dsem("misc")
    fw_in_v = fw_in.rearrange("(dc p) (two j c) -> p dc two j c", p=128, two=2, c=128)
    for j in range(NF):
        for two in range(2):
            _prep_cast(P, prep, wb1[j][:, :, two * 128:(two + 1) * 128], fw_in_v[:, :, two, j, :])
    fw_out_v = fw_out.rearrange("(fc p) (dn c) -> p fc dn c", p=128, c=128)
    for dcn in range(DC):
        _prep_cast(P, prep, wb2[dcn], fw_out_v[:, :, dcn, :])
    w_out_v = w_out.rearrange("(dc p) (t c) -> p dc t c", p=128, c=256)
    for t in range(8):
        _prep_cast(P, prep, wbo[t], w_out_v[:, :, t, :])
    prep_total = (prep.h, prep.count)
    prep_tok = lambda: prep_total
    st["w1"] = WRing(P, "w1b", [128, DC, 256], 3, prep_tok)
    st["w2"] = WRing(P, "w2b", [128, NF, 128], 2, prep_tok)
    modv = P.sb("modv", [128, 4 * DC], F32)
    ng3 = P.sb("ng3", [128, DC], F32)
    fn = P.sb("fn", [128, DC], F32)
    A3 = P.sb("A3", [128, DC], F32)
    G3 = P.sb("G3", [128, DC], F32)
    cat = P.sb("cat", [128, DC, T], BF16)
    ofin = [P.sb(f"ofin{i}", [128, T], F32) for i in range(2)]
    stage = {
        "cact": P.sb("cact", [128, DC], F32), "ct_raw": P.sb("ct_raw", [128, DC], F32),
        "bm": P.sb("bm", [128, 4 * DC], F32), "sg": P.sb("sgc", [128, DC], F32),
        "wbufs": [st["xt"][:, :, 0:256], st["xt"][:, :, 256:512]],
    }
    P.memset("dve", st["ones_bf"][:, :], 1.0)
    P.dma("pool", ng3[:], ng3T, misc)
    P.dma("pool", fn[:], fnT, misc)
    mod_tok = _emit_mod(P, cfg, 4, cT, wmod, bmodT, modv, stage, st["y_ps"][0], misc)
    misc_tok = (misc.h, misc.count)
    P.wait("dve", misc_tok)
    P.wait("dve", mod_tok)
    g2, sh3, sc3, g3 = [modv[:, i * DC:(i + 1) * DC] for i in range(4)]
    P.stt(A3[:], sc3, 1.0, ng3[:], ALU.add, ALU.mult)
    const_tok = P.ts(G3[:], g3, 0.5, None, ALU.mult, sig=True)

    xld = P.dsem("xld")
    cld = P.dsem("cld")
    xst = P.dsem("xst")
    ost = P.dsem("ost")
    xT_v = xT.rearrange("(dc p) t -> p dc t", p=128)
    xTo_v = xT_out.rearrange("(dc p) t -> p dc t", p=128)
    oT_v = oT.rearrange("(dc p) t -> p dc t", p=128)
    cat_v = catT.rearrange("(dc p) t -> p dc t", p=128)
    xt = st["xt"]
    xt_free = [mod_tok]
    cat_free = None
    of_free = [None, None]
    for it in range(NT):
        c0, c1 = it * T, (it + 1) * T
        for tk in xt_free:
            P.wait("pool", tk)
        x_ld = P.dma("pool", xt[:], xT_v[:, :, c0:c1], xld)
        P.wait("pool", cat_free)
        c_ld = P.dma("pool", cat[:], cat_v[:, :, c0:c1], cld)
        for t in range(8):
            st["w1"].load(wbo[t])
        P.wait("dve", x_ld)
        P.wait("dve", const_tok)
        P.wait("pe", c_ld)
        xtok = None
        pt = None
        for wt_i in range(8):
            b, wt = st["w1"].acquire()
            for half in range(2):
                dcn = 2 * wt_i + half
                s = dcn % 2
                y_ps = st["y_ps"][s]
                P.wait("pe", st["y_free"][s])
                for ncx in range(DC):
                    pt = P.mm(y_ps[:, :], wt[:, ncx, half * 128:(half + 1) * 128], cat[:, ncx, :],
                              start=(ncx == 0), stop=(ncx == DC - 1), sig=(ncx == DC - 1))
                P.wait("dve", pt)
                xtok = P.stt(xt[:, dcn, :], y_ps[:, :], g2[:, dcn:dcn + 1], xt[:, dcn, :], ALU.mult, ALU.add, sig=True)
                st["y_free"][s] = xtok
            st["w1"].release(b, pt)
        cat_free = pt
        xtok = _emit_ffn(P, cfg, st, lambda j: wb1[j], lambda d: wb2[d], A3, sh3, G3, xtok)
        P.wait("pool", xtok)
        xs_tok = P.dma("pool", xTo_v[:, :, c0:c1], xt[:], xst)
        sq_tok = _emit_norm(P, cfg, xt, st["sq"], st["ss_ps"], st["rstd"], st["ones_bf"], xtok, RMS_EPS)
        last = None
        for dc in range(DC):
            oi = dc % 2
            P.wait("dve", of_free[oi])
            last = P.stt(ofin[oi][:, :], xt[:, dc, :], fn[:, dc:dc + 1], st["rstd"][:, :], ALU.mult, ALU.mult, sig=True)
            P.wait("pool", last)
            of_free[oi] = P.dma("pool", oT_v[:, dc, c0:c1], ofin[oi][:, :], ost)
        xt_free = [xs_tok, last, sq_tok]
    P.do("pool", lambda e: e.wait_ge(xst.h, xst.count))
    P.do("pool", lambda e: e.wait_ge(ost.h, ost.count))
    P.emit()
    return nc


def build_p2(cfg, dbg=False):
    S, B = cfg.S, cfg.B
    NTK = B * S
    NQT = cfg.NQT
    NKT = S // 128
    nc = bass.Bass("TRN2", target_bir_lowering=False)
    dt = lambda n, s, d, k: nc.dram_tensor(n, s, d, kind=k).ap()
    QT = dt("QT", [128, NTK], BF16, "ExternalInput")
    KT = dt("KT", [128, NTK], BF16, "ExternalInput")
    V = dt("V", [NTK, 128], BF16, "ExternalInput")
    PMT = dt("PMT", [128, NTK], BF16, "ExternalInput")
    lamp = dt("lamp", [128, 256], F32, "ExternalInput")
    smin = dt("smin", [128, 4], F32, "ExternalInput")
    coef = dt("coef", [128, 4], F32, "ExternalInput")
    corr = dt("corr", [128, 16], F32, "ExternalInput")
    tri = dt("tri", [128, 128], BF16, "ExternalInput")
    YA = dt("YA", [128, NTK], BF16, "ExternalOutput")
    YP = dt("YP", [128, NTK], BF16, "ExternalOutput")
    DBG = dt("DBG", [6, 128, 512], F32, "ExternalOutput") if dbg else None

    P = Prog(nc)
    qt = P.sb("qt", [128, S], BF16)
    kt_ = P.sb("kt", [128, S], BF16)
    vt = P.sb("vt", [128, NKT, 128], BF16)
    ones_bf = P.sb("ones_bf", [128, 128], BF16)
    ones_f = P.sb("ones_f", [128, 128], F32)
    tri_sb = P.sb("tri_sb", [128, 128], BF16)
    lam_sb = P.sb("lam_sb", [128, 256], F32)
    sm = P.sb("sm", [128, 16], F32)
    coef_sb = P.sb("coef_sb", [128, 4], F32)
    corr_sb = P.sb("corr_sb", [128, 16], F32)
    pb = [[P.sb(f"pb{c}{i}", [128, 512], BF16) for i in range(2)] for c in range(2)]
    rl = [P.sb(f"rl{c}", [128, 512], F32) for c in range(2)]
    a0 = P.sb("a0", [128, 512], F32)
    a1 = P.sb("a1", [128, 512], F32)
    sqf = P.sb("sqf", [128, 512], F32)
    rs = P.sb("rs", [128, 512], F32)
    yb = [P.sb(f"yb{i}", [128, 512], BF16) for i in range(2)]
    PC = 2048
    pmx = P.sb("pmx", [128, 16 + PC], BF16)
    s2 = P.sb("s2", [128, 16 + PC], F32)
    s4 = P.sb("s4", [128, 16 + PC], F32)
    s8 = P.sb("s8", [128, 16 + PC], F32)
    s16 = P.sb("s16", [128, 16 + PC], F32)
    acc = P.sb("acc", [128, PC], F32)
    ypb = P.sb("ypb", [128, PC], BF16)
    s_ps = [[P.ps(f"s_ps{c}{i}", [128, 512]) for i in range(2)] for c in range(2)]
    o_ps = [P.ps(f"o_ps{c}", [128, 512]) for c in range(2)]
    l_ps = [P.ps(f"l_ps{c}", [128, 512]) for c in range(2)]

    misc = P.dsem("misc")
    P.memset("dve", ones_bf[:, :], 1.0)
    P.memset("dve", ones_f[:, :], 1.0)
    P.dma("pool", lam_sb[:], lamp, misc)
    P.dma("pool", tri_sb[:], tri, misc)
    P.dma("pool", coef_sb[:], coef, misc)
    P.dma("pool", corr_sb[:], corr, misc)
    P.dma("pool", sm[:, 0:4], smin, misc)
    misc_tok = (misc.h, misc.count)
    P.wait("dve", misc_tok)
    P.tt(a0[:, 0:64], lam_sb[:, 0:64], lam_sb[:, 64:128], ALU.mult)
    k = P.tt(a0[:, 64:128], lam_sb[:, 128:192], lam_sb[:, 192:256], ALU.mult, sig=True)
    P.wait("dve", k)
    P.do("dve", lambda e: e.reduce_sum(out=sm[:, 4:5], in_=a0[:, 0:64], axis=AX.X))
    d_tok = P.do("dve", lambda e: e.reduce_sum(out=sm[:, 5:6], in_=a0[:, 64:128], axis=AX.X), sig=True)
    P.wait("act", d_tok)
    a_tok = P.act(sm[:, 4:6], sm[:, 4:6], AF.Exp, sig=True)
    P.wait("dve", a_tok)
    k = P.tt(sm[:, 6:7], sm[:, 4:5], sm[:, 5:6], ALU.subtract, sig=True)
    P.wait("dve", k)
    k = P.tt(sm[:, 9:10], sm[:, 6:7], sm[:, 1:2], ALU.add, sig=True)
    P.wait("dve", k)
    k = P.ts(sm[:, 7:8], sm[:, 9:10], -1.0, None, ALU.mult, sig=True)
    k = P.ts(sm[:, 10:11], sm[:, 1:2], -1.0, 1.0, ALU.mult, ALU.add, sig=True)
    P.wait("dve", k)
    k = P.tt(sm[:, 8:9], sm[:, 10:11], sm[:, 0:1], ALU.mult, sig=True)
    P.wait("dve", k)
    neg_lam = sm[:, 7:8]
    gscale = sm[:, 8:9]

    pld = P.dsem("pld")
    pst = P.dsem("pst")
    pm_free = None
    yp_free = None
    for b in range(B):
        for ch in range(S // PC):
            g0 = b * S + ch * PC
            P.wait("pool", pm_free)
            if ch == 0:
                k = P.memset("dve", pmx[:, 0:16], 0.0, sig=True)
                P.wait("dve", k)
                ld = P.dma("pool", pmx[:, 16:16 + PC], PMT[:, g0:g0 + PC], pld)
            else:
                ld = P.dma("pool", pmx[:, :], PMT[:, g0 - 16:g0 + PC], pld)
            P.wait("dve", ld)
            W = 16 + PC
            P.tt(s2[:, 1:W], pmx[:, 1:W], pmx[:, 0:W - 1], ALU.add)
            P.tt(s4[:, 3:W], s2[:, 3:W], s2[:, 1:W - 2], ALU.add)
            P.tt(s8[:, 7:W], s4[:, 7:W], s4[:, 3:W - 4], ALU.add)
            P.tt(s16[:, 15:W], s8[:, 15:W], s8[:, 7:W - 8], ALU.add)
            P.ts(acc[:, :], s2[:, 16:W], coef_sb[:, 0:1], None, ALU.mult)
            P.stt(acc[:, :], s4[:, 16:W], coef_sb[:, 1:2], acc[:, :], ALU.mult, ALU.add)
            P.stt(acc[:, :], s8[:, 16:W], coef_sb[:, 2:3], acc[:, :], ALU.mult, ALU.add)
            P.stt(acc[:, :], s16[:, 16:W], coef_sb[:, 3:4], acc[:, :], ALU.mult, ALU.add)
            if ch == 0:
                k = P.tt(acc[:, 0:16], acc[:, 0:16], corr_sb[:, :], ALU.mult, sig=True)
                P.wait("dve", k)
            pm_free = P.tt(acc[:, :], acc[:, :], pmx[:, 16:W], ALU.subtract, sig=True)
            P.wait("dve", yp_free)
            y_tok = P.ts(ypb[:, :], acc[:, :], sm[:, 2:3], sm[:, 3:4], ALU.add, ALU.mult, sig=True)
            P.wait("pool", y_tok)
            yp_free = P.dma("pool", YP[:, g0:g0 + PC], ypb[:, :], pst)

    qld = P.dsem("qld")
    yst = P.dsem("yst")
    scale = 64 ** -0.5
    e_tok = {}
    pv_tok = {}
    qkv_free = None
    yb_free = [None, None]
    fin_tok = None
    n = 0
    nq = 0
    for b in range(B):
        P.wait("pool", qkv_free)
        P.dma("pool", qt[:], QT[:, b * S:(b + 1) * S], qld)
        P.dma("pool", kt_[:], KT[:, b * S:(b + 1) * S], qld)
        v_v = V[b * S:(b + 1) * S, :].rearrange("(k p) d -> p k d", p=128)
        q_tok = None
        for vq in range(4):
            ks = slice(vq * NKT // 4, (vq + 1) * NKT // 4)
            q_tok = P.dma("pool", vt[:, ks, :], v_v[:, ks, :], qld)
        P.wait("pe", q_tok)
        for i in range(NQT):
            nk = 4 * (i + 1)
            its = []
            for kt in range(nk):
                r = kt - 4 * i
                cs = 128 * max(r, 0)
                its.append((kt, cs, r >= 0))

            def emit_S(idx):
                kt, cs, diag = its[idx]
                m = n + idx
                bf = m % 2
                P.wait("pe", e_tok.get(m - 2))
                pt = None
                for c in range(2):
                    pt = P.mm(s_ps[c][bf][:, cs:512], kt_[64 * c:64 * c + 64, kt * 128:(kt + 1) * 128],
                              qt[64 * c:64 * c + 64, i * 512 + cs:(i + 1) * 512], start=True, stop=True, sig=(c == 1))
                return pt

            s_tok = {0: emit_S(0)}
            for idx in range(nk):
                kt, cs, diag = its[idx]
                m = n + idx
                bf = m % 2
                if idx + 1 < nk:
                    s_tok[idx + 1] = emit_S(idx + 1)
                P.wait("act", s_tok[idx])
                P.wait("act", pv_tok.get(m - 2))
                at = None
                for c in range(2):
                    at = P.act(pb[c][bf][:, cs:512], s_ps[c][bf][:, cs:512], AF.Exp, scale=scale, sig=(c == 1))
                e_tok[m] = at
                ready = at
                if diag:
                    P.wait("dve", at)
                    dtk = None
                    for c in range(2):
                        dtk = P.tt(pb[c][bf][:, cs:cs + 128], pb[c][bf][:, cs:cs + 128], tri_sb[:, :], ALU.mult, sig=(c == 1))
                    ready = dtk
                P.wait("pe", ready)
                if idx == 0:
                    P.wait("pe", fin_tok)
                pt = None
                for c in range(2):
                    P.mm(o_ps[c][:, cs:512], vt[:, kt, :], pb[c][bf][:, cs:512], start=(idx == 0), stop=(idx == nk - 1))
                for c in range(2):
                    pt = P.mm(l_ps[c][:, cs:512], ones_bf[:, :], pb[c][bf][:, cs:512], start=(idx == 0), stop=(idx == nk - 1),
                              sig=(c == 1))
                pv_tok[m] = pt
            n += nk
            last_pv = pv_tok[n - 1]
            P.wait("dve", last_pv)
            P.do("dve", lambda e, c=0: e.reciprocal(out=rl[0][:, :], in_=l_ps[0][:, :]))
            P.do("dve", lambda e, c=1: e.reciprocal(out=rl[1][:, :], in_=l_ps[1][:, :]))
            P.tt(a0[:, :], o_ps[0][:, :], rl[0][:, :], ALU.mult)
            P.tt(a1[:, :], o_ps[1][:, :], rl[1][:, :], ALU.mult)
            dtk = P.stt(a0[:, :], a1[:, :], neg_lam, a0[:, :], ALU.mult, ALU.add, sig=True)
            P.wait("act", dtk)
            atk = P.act(sqf[:, :], a0[:, :], AF.Square, sig=True)
            P.wait("pe", atk)
            ptk = P.mm(l_ps[0][:, :], ones_f[:, :], sqf[:, :], start=True, stop=True, sig=True)
            P.wait("dve", ptk)
            d_tok = P.ts(rs[:, :], l_ps[0][:, :], 1.0 / 128, SUBLN_EPS, ALU.mult, ALU.add, sig=True)
            P.wait("act", d_tok)
            s_tok = P.act(rs[:, :], rs[:, :], AF.Sqrt, sig=True)
            P.wait("dve", s_tok)
            fin_tok = P.do("dve", lambda e: e.reciprocal(out=rs[:, :], in_=rs[:, :]), sig=True)
            yi = nq % 2
            nq += 1
            P.wait("dve", yb_free[yi])
            ytk = P.stt(yb[yi][:, :], a0[:, :], gscale, rs[:, :], ALU.mult, ALU.mult, sig=True)
            P.wait("pool", ytk)
            yb_free[yi] = P.dma("pool", YA[:, b * S + i * 512:b * S + (i + 1) * 512], yb[yi][:, :], yst)
            if dbg and b == 0 and i == 0:
                for di, tsr in enumerate([a0, rs, rl[0], rl[1], sqf, a1]):
                    P.dma("pool", DBG[di], tsr[:, :], yst)
                P.do("pool", lambda e, v=yst.count: e.wait_ge(yst.h, v))
        qkv_free = pv_tok[n - 1]
    P.do("pool", lambda e: e.wait_ge(pst.h, pst.count))
    P.do("pool", lambda e: e.wait_ge(yst.h, yst.count))
    P.emit()
    return nc


_CACHE = {}
_DBG = None


def _get(name, cfg, builder):
    key = (name, cfg.D, cfg.F, cfg.S, cfg.B, cfg.T)
    if key not in _CACHE:
        _CACHE[key] = builder(cfg)
    return _CACHE[key]


def _chunkT(v, DC):
    return np.ascontiguousarray(np.asarray(v, dtype=np.float32).reshape(-1, 128).T)


def _rope_tables(cfg, j):
    TOK = cfg.TOK
    pos = (np.arange(TOK, dtype=np.float32) + np.float32(j * TOK)).astype(np.float32)
    inv_freq = (np.float32(10000.0) ** (-(np.arange(0, 64, 2, dtype=np.float32)) / np.float32(64))).astype(np.float32)
    ang = (pos[:, None] * inv_freq[None, :]).astype(np.float32)
    cos, sin = np.cos(ang).astype(np.float32), np.sin(ang).astype(np.float32)
    p = np.arange(128)
    i = p % 64
    f = i % 32
    sign = np.where(i < 32, -1.0, 1.0).astype(np.float32)
    C = np.ascontiguousarray(cos[:, f].T)
    Sg = np.ascontiguousarray((sin[:, f] * sign[None, :]).T)
    return C, Sg


def _run(nc, in_maps):
    res = run_bass_kernel_spmd(nc, in_maps, core_ids=list(range(NCORE)))
    return res.results


def kernel_cfg(cfg, x, c, w_mod, b_mod, norm_ffn1, ffn1_w_in, ffn1_w_out, norm_mix, w_in, pool_w, pool_b, pool_scale,
               diff_lambda, diff_subln, w_out, norm_ffn2, ffn2_w_in, ffn2_w_out, final_norm):
    D, DC, TOK, S, B, L = cfg.D, cfg.DC, cfg.TOK, cfg.S, cfg.B, cfg.L
    f32 = lambda a: np.asarray(a, dtype=np.float32)
    x = f32(x)
    c = f32(c)
    p1 = _get("p1", cfg, build_p1)
    p2 = _get("p2", cfg, build_p2)
    p3 = _get("p3", cfg, build_p3)
    per_b = NCORE // B
    xTs = []
    for cc in range(NCORE):
        b, j = cc // per_b, cc % per_b
        xTs.append(np.ascontiguousarray(x[b, j * TOK:(j + 1) * TOK, :].T))
    cTs = [_chunkT(c[cc // per_b], DC) for cc in range(NCORE)]
    ropes = [_rope_tables(cfg, cc % per_b) for cc in range(NCORE)]
    p = np.arange(128)
    partner = np.where((p % 64) < 32, p + 32, p - 32)
    permM = np.zeros((128, 128), np.float32)
    permM[partner, p] = 1.0
    tri = (p[None, :] >= p[:, None]).astype(np.float32).astype(ml_dtypes.bfloat16)
    fin = None
    for l in range(L):
        lambda_init = 0.8 - 0.6 * math.exp(-0.3 * l)
        wm = f32(w_mod[l])
        bm = f32(b_mod[l])
        common = {
            "wmod": np.ascontiguousarray(wm[:, 0:5 * D]), "bmodT": _chunkT(bm[0:5 * D], DC),
            "ng1T": _chunkT(norm_ffn1[l], DC), "ng2T": _chunkT(norm_mix[l], DC),
            "fw_in": f32(ffn1_w_in[l]), "fw_out": f32(ffn1_w_out[l]), "w_in": f32(w_in[l]),
            "pool_w": f32(pool_w[l]), "permM": permM,
        }
        in_maps = []
        for cc in range(NCORE):
            m = dict(common)
            m.update({"xT": xTs[cc], "cT": cTs[cc], "ropeC": ropes[cc][0], "ropeS": ropes[cc][1]})
            in_maps.append(m)
        r1 = _run(p1, in_maps)
        if _DBG is not None and l == 0:
            _DBG["r1"] = [{k: np.asarray(v) for k, v in r.items()} for r in r1]
        xTs = [np.asarray(r1[cc]["xT_out"]) for cc in range(NCORE)]
        in_maps = []
        lamp = np.ascontiguousarray(np.broadcast_to(f32(diff_lambda[l]).reshape(1, 256), (128, 256)))
        for h in range(NCORE):
            sl = slice(h * 128, (h + 1) * 128)
            w = POOL_WINDOWS[h // 2]
            coef = np.zeros((128, 4), np.float32)
            coef[:, POOL_WINDOWS.index(w)] = 1.0 / w
            corr = np.broadcast_to((w / np.minimum(np.arange(16) + 1, w)).astype(np.float32)[None, :], (128, 16))
            in_maps.append({
                "QT": np.ascontiguousarray(np.concatenate([np.asarray(r1[cc]["QT"])[sl, :] for cc in range(NCORE)], axis=1)),
                "KT": np.ascontiguousarray(np.concatenate([np.asarray(r1[cc]["KT"])[sl, :] for cc in range(NCORE)], axis=1)),
                "V": np.ascontiguousarray(np.concatenate([np.asarray(r1[cc]["Vo"])[:, sl] for cc in range(NCORE)], axis=0)),
                "PMT": np.ascontiguousarray(np.concatenate([np.asarray(r1[cc]["PMT"])[sl, :] for cc in range(NCORE)], axis=1)),
                "lamp": lamp,
                "smin": np.ascontiguousarray(np.stack([f32(diff_subln[l]).reshape(128), np.full((128,), lambda_init, np.float32),
                                                       f32(pool_b[l]).reshape(-1)[sl], f32(pool_scale[l])[sl]], axis=1)),
                "coef": coef, "corr": np.ascontiguousarray(corr), "tri": tri,
            })
        del r1
        r2 = _run(p2, in_maps)
        if _DBG is not None and l == 0:
            _DBG["r2"] = [{k: np.asarray(v) for k, v in r.items()} for r in r2]
        in_maps = []
        common = {
            "wmod": np.ascontiguousarray(wm[:, 5 * D:9 * D]), "bmodT": _chunkT(bm[5 * D:9 * D], DC),
            "ng3T": _chunkT(norm_ffn2[l], DC), "fnT": _chunkT(final_norm, DC),
            "fw_in": f32(ffn2_w_in[l]), "fw_out": f32(ffn2_w_out[l]), "w_out": f32(w_out[l]),
        }
        for cc in range(NCORE):
            ts = slice(cc * TOK, (cc + 1) * TOK)
            cat = np.concatenate([np.asarray(r2[h]["YP"])[:, ts] for h in range(NCORE)]
                                 + [np.asarray(r2[h]["YA"])[:, ts] for h in range(NCORE)], axis=0)
            m = dict(common)
            m.update({"xT": xTs[cc], "cT": cTs[cc], "catT": np.ascontiguousarray(cat)})
            in_maps.append(m)
        del r2
        r3 = _run(p3, in_maps)
        if _DBG is not None and l == 0:
            _DBG["r3"] = [{k: np.asarray(v) for k, v in r.items()} for r in r3]
        xTs = [np.asarray(r3[cc]["xT_out"]) for cc in range(NCORE)]
        if l == L - 1:
            fin = [np.asarray(r3[cc]["oT"]) for cc in range(NCORE)]
        del r3
    out = np.empty((B, S, D), np.float32)
    for cc in range(NCORE):
        b, j = cc // per_b, cc % per_b
        out[b, j * TOK:(j + 1) * TOK, :] = fin[cc].T
    return out


def kernel(**inputs):
    cfg = Cfg()
    return kernel_cfg(cfg, **inputs)
```
